# Optimizing a Trainium2 kernel written in Bass

```python
import math
import jax, jax.numpy as jnp
from jax import lax
import numpy as np

D_MODEL = 1024
BATCH = 8
SEQ = 8192
DEPTH = 1

SSD_EXPAND = 2
SSD_D_INNER = SSD_EXPAND * D_MODEL
SSD_HEAD_DIM = 64
SSD_N_HEADS = SSD_D_INNER // SSD_HEAD_DIM
SSD_N_GROUPS = 4
SSD_HEADS_PER_GROUP = SSD_N_HEADS // SSD_N_GROUPS
SSD_D_STATE = 128
SSD_CONV_WIDTH = 4
SSD_CONV_DIM = SSD_D_INNER + 2 * SSD_N_GROUPS * SSD_D_STATE
DT_MIN = 0.001
DT_MAX = 0.1
A_MIN = 1.0
A_MAX = 16.0
RET_N_HEADS = 8
RET_QK_HEAD_DIM = D_MODEL // RET_N_HEADS
RET_V_HEAD_DIM = 2 * RET_QK_HEAD_DIM
RET_QK_DIM = RET_N_HEADS * RET_QK_HEAD_DIM
RET_V_DIM = RET_N_HEADS * RET_V_HEAD_DIM
ROPE_BASE = 10000.0
MAX_POS_OFFSET = 4096
CHUNK = 128
FFN_DIM = 2816
FFN_RESIDUAL_WEIGHT = 0.5
N_MOD = 9
EPS = 1e-6
MIX_IN_SIZES = (SSD_D_INNER, SSD_CONV_DIM, SSD_N_HEADS, RET_QK_DIM, RET_QK_DIM,
                RET_V_DIM, RET_V_DIM, D_MODEL, D_MODEL)
MIX_IN_DIM = sum(MIX_IN_SIZES)

kernel_name = "hybrid_ssd_retention_macaron_adaln"


def rms_normalize(x):
    xf = x.astype(jnp.float32)
    return (xf * lax.rsqrt(jnp.mean(xf * xf, axis=-1, keepdims=True) + EPS)).astype(x.dtype)


def rms_norm(x, w):
    return rms_normalize(x) * w


def modulate(x, shift, scale):
    return x * (1.0 + scale[:, None, :]) + shift[:, None, :]


def swiglu(x, w_in, w_out):
    gate, up = jnp.split(x @ w_in, 2, axis=-1)
    return (jax.nn.silu(gate) * up) @ w_out


def rotary(x, positions):
    half = x.shape[-1] // 2
    inv_freq = ROPE_BASE ** (-jnp.arange(half, dtype=jnp.float32) / half)
    ang = positions.astype(jnp.float32)[..., None] * inv_freq
    cos = jnp.cos(ang)[:, :, None, :]
    sin = jnp.sin(ang)[:, :, None, :]
    xf = x.astype(jnp.float32)
    x1, x2 = xf[..., :half], xf[..., half:]
    return jnp.concatenate([x1 * cos - x2 * sin, x2 * cos + x1 * sin], axis=-1).astype(x.dtype)


def causal_depthwise_conv(x, w, b):
    k, ch = w.shape
    y = lax.conv_general_dilated(x, w[:, None, :].astype(x.dtype), window_strides=(1,),
                                 padding=[(k - 1, 0)], dimension_numbers=('NWC', 'WIO', 'NWC'),
                                 feature_group_count=ch)
    return y + b


def segsum_exp(a):
    cs = jnp.cumsum(a, axis=-1)
    diff = cs[..., :, None] - cs[..., None, :]
    n = a.shape[-1]
    mask = jnp.tril(jnp.ones((n, n), dtype=bool))
    return jnp.exp(jnp.where(mask, diff, -jnp.inf))


def exclusive_chunk_scan(states, decays):
    def step(carry, inp):
        s, d = inp
        return carry * d[..., None, None] + s, carry
    init = jnp.zeros(states.shape[1:], jnp.float32)
    _, prev = lax.scan(step, init, (states.astype(jnp.float32), decays.astype(jnp.float32)))
    return prev


def ssd_branch(z, xbc, dt_raw, conv_w, conv_b, dt_bias, a_log, d_skip, norm_w):
    bsz, s, _ = z.shape
    nc = s // CHUNK
    g_, r_, p_, n_ = SSD_N_GROUPS, SSD_HEADS_PER_GROUP, SSD_HEAD_DIM, SSD_D_STATE
    xbc = jax.nn.silu(causal_depthwise_conv(xbc, conv_w, conv_b))
    xs, bm, cm = jnp.split(xbc, [SSD_D_INNER, SSD_D_INNER + g_ * n_], axis=-1)
    xs = xs.reshape(bsz, nc, CHUNK, g_, r_, p_)
    bm = bm.reshape(bsz, nc, CHUNK, g_, n_)
    cm = cm.reshape(bsz, nc, CHUNK, g_, n_)
    dt = jax.nn.softplus(dt_raw.astype(jnp.float32) + dt_bias.astype(jnp.float32))
    dt = dt.reshape(bsz, nc, CHUNK, g_, r_)
    a_head = -jnp.exp(a_log.astype(jnp.float32)).reshape(g_, r_)
    a = jnp.transpose(dt * a_head, (0, 3, 4, 1, 2))
    a_cs = jnp.cumsum(a, axis=-1)
    xdt = xs * dt[..., None]
    cb = jnp.einsum('bclgn,bcsgn->bgcls', cm, bm)
    scores = cb[:, :, None] * segsum_exp(a)
    y_diag = jnp.einsum('bgrcls,bcsgrp->bclgrp', scores, xdt)
    decay_to_end = jnp.exp(a_cs[..., -1:] - a_cs)
    chunk_states = jnp.einsum('bclgn,bgrcl,bclgrp->bcgrpn', bm, decay_to_end, xdt)
    chunk_decay = jnp.exp(a_cs[..., -1])
    prev_states = exclusive_chunk_scan(jnp.moveaxis(chunk_states, 1, 0),
                                       jnp.moveaxis(chunk_decay, -1, 0))
    y_off = jnp.einsum('bclgn,cbgrpn,bgrcl->bclgrp', cm, prev_states, jnp.exp(a_cs))
    y = y_diag + y_off + xs * d_skip.reshape(g_, r_)[..., None]
    y = y.reshape(bsz, s, SSD_D_INNER).astype(z.dtype)
    yg = (y * jax.nn.silu(z)).reshape(bsz, s, g_, SSD_D_INNER // g_)
    return rms_normalize(yg).reshape(bsz, s, SSD_D_INNER) * norm_w


def retention_branch(q, k, v, g, positions, norm_w):
    bsz, s, _ = q.shape
    nc = s // CHUNK
    h_, dk, dv = RET_N_HEADS, RET_QK_HEAD_DIM, RET_V_HEAD_DIM
    q = rotary(q.reshape(bsz, s, h_, dk), positions)
    k = rotary(k.reshape(bsz, s, h_, dk), positions) * (dk ** -0.5)
    qc = q.reshape(bsz, nc, CHUNK, h_, dk)
    kc = k.reshape(bsz, nc, CHUNK, h_, dk)
    vc = v.reshape(bsz, nc, CHUNK, h_, dv)
    log_gamma = jnp.log1p(-jnp.exp2(-5.0 - jnp.arange(h_, dtype=jnp.float32)))
    idx = jnp.arange(CHUNK, dtype=jnp.float32)
    rel = idx[:, None] - idx[None, :]
    decay_mask = jnp.where(rel >= 0, jnp.exp(log_gamma[:, None, None] * jnp.maximum(rel, 0.0)), 0.0)
    scores = jnp.einsum('bclhd,bcshd->bchls', qc, kc) * decay_mask
    y_intra = jnp.einsum('bchls,bcshe->bclhe', scores, vc)
    k_decay = jnp.exp(log_gamma[None, :] * (CHUNK - 1.0 - idx)[:, None])
    chunk_states = jnp.einsum('bcshd,sh,bcshe->bchde', kc, k_decay, vc)
    chunk_decay = jnp.broadcast_to(jnp.exp(log_gamma * CHUNK), (nc, bsz, h_))
    prev_states = exclusive_chunk_scan(jnp.moveaxis(chunk_states, 1, 0), chunk_decay)
    q_decay = jnp.exp(log_gamma[None, :] * (idx + 1.0)[:, None])
    y_inter = jnp.einsum('bclhd,cbhde,lh->bclhe', qc, prev_states, q_decay)
    y = (y_intra + y_inter).reshape(bsz, s, h_, dv).astype(v.dtype)
    y = rms_normalize(y).reshape(bsz, s, RET_V_DIM) * norm_w
    return y * jax.nn.silu(g)


def hybrid_mixer(u, positions, w_in, gate_b, ssd_conv_w, ssd_conv_b, ssd_dt_bias, ssd_a_log,
                 ssd_d, ssd_norm_w, ret_norm_w, w_br_ssd, w_br_ret, w_out):
    proj = u @ w_in
    splits = np.cumsum(MIX_IN_SIZES)[:-1].tolist()
    z, xbc, dt_raw, q, k, v, g, gate_ssd_raw, gate_ret_raw = jnp.split(proj, splits, axis=-1)
    gate_b_ssd, gate_b_ret = jnp.split(gate_b, 2, axis=-1)
    y_ssd = ssd_branch(z, xbc, dt_raw, ssd_conv_w, ssd_conv_b, ssd_dt_bias, ssd_a_log,
                       ssd_d, ssd_norm_w) @ w_br_ssd
    y_ret = retention_branch(q, k, v, g, positions, ret_norm_w) @ w_br_ret
    gate_ssd = jax.nn.sigmoid(gate_ssd_raw + gate_b_ssd)
    gate_ret = jax.nn.sigmoid(gate_ret_raw + gate_b_ret)
    return (gate_ssd * y_ssd + gate_ret * y_ret) @ w_out


def setup_inputs(seed: int = 0) -> dict:
    key = jax.random.key(seed)
    ks = iter(jax.random.split(key, 40))
    f32 = jnp.float32
    L = DEPTH

    def dense(shape, fan_in, scale=1.0):
        return jax.random.normal(next(ks), shape, f32) * (scale * fan_in ** -0.5)

    def gain(shape):
        return 1.0 + 0.02 * jax.random.normal(next(ks), shape, f32)

    def bias(shape):
        return 0.01 * jax.random.normal(next(ks), shape, f32)

    x = jax.random.normal(next(ks), (BATCH, SEQ, D_MODEL), f32)
    c = jax.random.normal(next(ks), (BATCH, D_MODEL), f32)
    offset = jax.random.randint(next(ks), (BATCH, 1), 0, MAX_POS_OFFSET, dtype=jnp.int32)
    positions = (offset + jnp.arange(SEQ, dtype=jnp.int32)[None, :]).astype(jnp.int32)
    ada_w = dense((L, D_MODEL, N_MOD * D_MODEL), D_MODEL, 0.5)
    ada_b = bias((L, N_MOD * D_MODEL))
    norm_ffn1_w = gain((L, D_MODEL))
    ffn1_w_in = dense((L, D_MODEL, 2 * FFN_DIM), D_MODEL)
    ffn1_w_out = dense((L, FFN_DIM, D_MODEL), FFN_DIM)
    norm_mix_w = gain((L, D_MODEL))
    mix_w_in = dense((L, D_MODEL, MIX_IN_DIM), D_MODEL)
    mix_gate_b = bias((L, 2 * D_MODEL))
    ssd_conv_w = dense((L, SSD_CONV_WIDTH, SSD_CONV_DIM), SSD_CONV_WIDTH)
    ssd_conv_b = bias((L, SSD_CONV_DIM))
    dt = jnp.exp(jax.random.uniform(next(ks), (L, SSD_N_HEADS), f32, math.log(DT_MIN), math.log(DT_MAX)))
    ssd_dt_bias = dt + jnp.log(-jnp.expm1(-dt))
    ssd_a_log = jnp.log(jax.random.uniform(next(ks), (L, SSD_N_HEADS), f32, A_MIN, A_MAX))
    ssd_d = gain((L, SSD_N_HEADS))
    ssd_norm_w = gain((L, SSD_D_INNER))
    ret_norm_w = gain((L, RET_V_DIM))
    w_br_ssd = dense((L, SSD_D_INNER, D_MODEL), SSD_D_INNER)
    w_br_ret = dense((L, RET_V_DIM, D_MODEL), RET_V_DIM)
    mix_w_out = dense((L, D_MODEL, D_MODEL), D_MODEL)
    norm_ffn2_w = gain((L, D_MODEL))
    ffn2_w_in = dense((L, D_MODEL, 2 * FFN_DIM), D_MODEL)
    ffn2_w_out = dense((L, FFN_DIM, D_MODEL), FFN_DIM)
    norm_final_w = gain((D_MODEL,))
    return {"x": x, "c": c, "positions": positions, "ada_w": ada_w, "ada_b": ada_b,
            "norm_ffn1_w": norm_ffn1_w, "ffn1_w_in": ffn1_w_in, "ffn1_w_out": ffn1_w_out,
            "norm_mix_w": norm_mix_w, "mix_w_in": mix_w_in, "mix_gate_b": mix_gate_b,
            "ssd_conv_w": ssd_conv_w, "ssd_conv_b": ssd_conv_b, "ssd_dt_bias": ssd_dt_bias,
            "ssd_a_log": ssd_a_log, "ssd_d": ssd_d, "ssd_norm_w": ssd_norm_w,
            "ret_norm_w": ret_norm_w, "w_br_ssd": w_br_ssd, "w_br_ret": w_br_ret,
            "mix_w_out": mix_w_out, "norm_ffn2_w": norm_ffn2_w, "ffn2_w_in": ffn2_w_in,
            "ffn2_w_out": ffn2_w_out, "norm_final_w": norm_final_w}


def reference(x, c, positions, ada_w, ada_b, norm_ffn1_w, ffn1_w_in, ffn1_w_out, norm_mix_w,
              mix_w_in, mix_gate_b, ssd_conv_w, ssd_conv_b, ssd_dt_bias, ssd_a_log, ssd_d,
              ssd_norm_w, ret_norm_w, w_br_ssd, w_br_ret, mix_w_out, norm_ffn2_w, ffn2_w_in,
              ffn2_w_out, norm_final_w):
    h = x
    c_act = jax.nn.silu(c)
    for l in range(DEPTH):
        mod = c_act @ ada_w[l] + ada_b[l]
        sh1, sc1, g1, sh2, sc2, g2, sh3, sc3, g3 = jnp.split(mod, N_MOD, axis=-1)
        u = modulate(rms_norm(h, norm_ffn1_w[l]), sh1, sc1)
        h = h + FFN_RESIDUAL_WEIGHT * g1[:, None, :] * swiglu(u, ffn1_w_in[l], ffn1_w_out[l])
        u = modulate(rms_norm(h, norm_mix_w[l]), sh2, sc2)
        h = h + g2[:, None, :] * hybrid_mixer(u, positions, mix_w_in[l], mix_gate_b[l],
                                              ssd_conv_w[l], ssd_conv_b[l], ssd_dt_bias[l],
                                              ssd_a_log[l], ssd_d[l], ssd_norm_w[l], ret_norm_w[l],
                                              w_br_ssd[l], w_br_ret[l], mix_w_out[l])
        u = modulate(rms_norm(h, norm_ffn2_w[l]), sh3, sc3)
        h = h + FFN_RESIDUAL_WEIGHT * g3[:, None, :] * swiglu(u, ffn2_w_in[l], ffn2_w_out[l])
    return rms_norm(h, norm_final_w)
```

```python
import math
import numpy as np
from contextlib import ExitStack
import ml_dtypes
import concourse.bass as bass
import concourse.mybir as mybir
from concourse.alu_op_type import AluOpType as ALU
from concourse.bass_utils import run_bass_kernel_spmd

F32 = mybir.dt.float32
BF16 = mybir.dt.bfloat16
I32 = mybir.dt.int32
AF = mybir.ActivationFunctionType

D = 1024
FF = 2816
NH = 32
EPS = 1e-6
T = 512
MIXW = 13344
O_Z, O_XS, O_B, O_C, O_DT, O_Q, O_K, O_V, O_G, O_GS, O_GR = (
    0, 2048, 4096, 4608, 5120, 5152, 6176, 7200, 9248, 11296, 12320)
TWO_PI = 2.0 * math.pi
CW1 = 6.28125
CW2 = TWO_PI - 6.28125


class Buf:
    __slots__ = ("name", "w", "r", "parent")

    def __init__(self, name="", parent=None):
        self.name = name
        self.w = None
        self.r = {}
        self.parent = parent


class Sched:
    SAME_ENGINE_SYNC = True

    def __init__(self, nc, stack):
        self.nc = nc
        self.stack = stack
        self.E = {"pe": nc.tensor, "act": nc.scalar, "dve": nc.vector,
                  "pool": nc.gpsimd, "sp": nc.sync}
        self.csem, self.cnt, self.seen = {}, {}, {}
        for e in self.E:
            self.csem[e] = stack.enter_context(nc.semaphore("c_" + e))
            self.cnt[e] = 0
            self.seen[e] = {}
        self.streams = {}
        self.nops = 0

    def stream(self, name):
        if name not in self.streams:
            self.streams[name] = [self.stack.enter_context(self.nc.semaphore("d_" + name)), 0]
        return self.streams[name]

    def _ext(self, reads, writes):
        ps = []
        for b in list(reads) + list(writes):
            if b.parent is not None and b.parent not in ps:
                ps.append(b.parent)
        return list(reads) + ps

    def _deps(self, eng, reads, writes):
        deps = {}

        def add(tok):
            if tok is None:
                return
            s, v = tok
            k = id(s)
            if k not in deps or deps[k][1] < v:
                deps[k] = (s, v)
        for b in reads:
            add(b.w)
        for b in writes:
            add(b.w)
            for t in b.r.values():
                add(t)
        e = self.E[eng]
        own = self.csem[eng]
        for k, (s, v) in deps.items():
            if s is own and (eng == "pe" or not self.SAME_ENGINE_SYNC):
                continue
            if self.seen[eng].get(k, 0) >= v:
                continue
            e.wait_ge(s, v)
            self.seen[eng][k] = v

    def op(self, eng, fn, reads=(), writes=(), inc=True):
        reads = self._ext(reads, writes)
        self._deps(eng, reads, writes)
        inst = fn(self.E[eng])
        self.nops += 1
        if inc:
            self.cnt[eng] += 1
            inst.then_inc(self.csem[eng], 1)
            tok = (self.csem[eng], self.cnt[eng])
        else:
            tok = (self.csem[eng], self.cnt[eng] + 1)
        for b in reads:
            b.r[eng] = tok
        for b in writes:
            b.w = tok
            b.r = {}
        return inst

    def dma(self, stream, pairs, reads=(), writes=(), eng="sp"):
        reads = self._ext(reads, writes)
        self._deps(eng, reads, writes)
        st = self.stream(stream)
        for (o, i) in pairs:
            self.E[eng].dma_start(out=o, in_=i).then_inc(st[0], 16)
            st[1] += 16
            self.nops += 1
        tok = (st[0], st[1])
        for b in reads:
            b.r["dma_" + stream] = tok
        for b in writes:
            b.w = tok
            b.r = {}

    def wait_all(self, eng, bufs):
        self._deps(eng, [], bufs)


def host_consts():
    k = np.arange(128)
    U = (k[:, None] <= k[None, :]).astype(np.float32)
    negm = np.where(k[None, :] < k[:, None], -30000.0, 0.0).astype(np.float32)
    rmask = (k[None, :] >= k[:, None]).astype(np.float32)
    hh = np.arange(8, dtype=np.float64)
    lg = np.log1p(-np.exp2(-5.0 - hh))
    qdec = np.exp(lg[None, :] * (k[:, None] + 1.0))
    kdec = np.exp(-lg[None, :] * (k[:, None] + 1.0)) * (128.0 ** -0.5)
    g128 = np.exp(lg * 128.0)
    half = 64
    invf = (10000.0 ** (-(np.arange(half, dtype=np.float32) / np.float32(half)))).astype(np.float32)
    cf = np.zeros((128, 128 * 4 + 64 + 16), np.float32)
    cf[:, 0:128] = np.eye(128)
    cf[:, 128:256] = U
    cf[:, 256:384] = 1.0
    cf[:, 384:512] = rmask
    cf[:, 512:576] = invf[None, :]
    cf[:, 576:584] = qdec
    cf[:, 584:592] = kdec
    cb = np.zeros((128, 128 + 512), np.float32)
    cb[:, 0:128] = np.eye(128)
    cb[:, 128:640] = np.tile(negm, (1, 4))
    return cf, cb.astype(ml_dtypes.bfloat16), [float(v) for v in g128]


def build(S, dbg=()):
    NT = S // T
    NCH = S // 128
    nc = bass.Bass("TRN2", target_bir_lowering=False)
    _, _, G128 = host_consts()

    def din(name, shape, dt=F32):
        return nc.dram_tensor(name, list(shape), dt, kind="ExternalInput").ap()

    def dscr(name, shape, dt=BF16):
        return nc.dram_tensor(name, list(shape), dt, kind="Internal").ap()

    x_d = din("x", [S, D])
    out_d = nc.dram_tensor("out", [S, D], F32, kind="ExternalOutput").ap()
    pos_d = din("pos", [128, NCH], I32)
    sm_d = din("smalls", [128, 8 + 72 + 32 + 16 + 96 + 24 + 96 + 16 + 16])
    cf_d = din("cf", [128, 592])
    cb_d = din("cb", [128, 640], BF16)
    ada_w = din("ada_w", [D, 9 * D])
    f1i = din("ffn1_w_in", [D, 2 * FF]); f1o = din("ffn1_w_out", [FF, D])
    f2i = din("ffn2_w_in", [D, 2 * FF]); f2o = din("ffn2_w_out", [FF, D])
    mwi = din("mix_w_in", [D, MIXW])
    wbs_d = din("w_br_ssd", [2048, D]); wbr_d = din("w_br_ret", [2048, D])
    wo_d = din("mix_w_out", [D, D])
    dbg_out = {}

    s_f1i = dscr("s_f1i", [11, 128, 4096]); s_f2i = dscr("s_f2i", [11, 128, 4096])
    s_f1o = dscr("s_f1o", [8, 128, FF]); s_f2o = dscr("s_f2o", [8, 128, FF])
    s_z = dscr("s_z", [4, 128, 4096])
    s_xs = dscr("s_xs", [4, 128, 4096])
    s_bc = dscr("s_bc", [4, 128, 2048])
    s_dt = dscr("s_dt", [128, 256])
    s_qk = dscr("s_qk", [8, 128, 2048])
    s_vg = dscr("s_vg", [8, 128, 4096])
    s_gs = dscr("s_gs", [2, 128, 4096]); s_gr = dscr("s_gr", [2, 128, 4096])
    s_bs = dscr("s_bs", [8, 128, 2048]); s_br = dscr("s_br", [8, 128, 2048])
    s_wo = dscr("s_wo", [2, 128, 4096])
    scrB = {}

    with ExitStack() as st:
        S_ = Sched(nc, st)

        uid = {"n": 0}

        def un(name):
            uid["n"] += 1
            return "t%d_%s" % (uid["n"], name)

        def sb(name, shape, dt=F32):
            return st.enter_context(nc.sbuf_tensor(un(name), list(shape), dt))

        cf = sb("cf", [128, 592]); cbt = sb("cbt", [128, 640], BF16)
        sm = sb("sm", [128, 376])
        B_const = Buf("const")
        ident = cf[:, 0:128]; Umat = cf[:, 128:256]; ones = cf[:, 256:384]; rmask = cf[:, 384:512]
        invf = cf[:, 512:576]; qdec = cf[:, 576:584]; kdec = cf[:, 584:592]
        identb = cbt[:, 0:128]; negm4 = cbt[:, 128:640]
        o = 0
        cvec = sm[:, o:o + 8]; o += 8
        adab = sm[:, o:o + 72]; o += 72
        nw = sm[:, o:o + 32]; o += 32
        gateb = sm[:, o:o + 16]; o += 16
        convw = sm[:, o:o + 96]; o += 96
        convb = sm[:, o:o + 24]; o += 24
        hv = sm[:, o:o + 96]; o += 96
        snw = sm[:, o:o + 16]; o += 16
        rnw = sm[:, o:o + 16]; o += 16
        mod = sb("mod", [128, 72]); B_mod = Buf("mod")
        vecs = sb("vecs", [128, 48]); B_vecs = Buf("vecs")
        Aneg = sb("Aneg", [128, 32])
        cact = sb("cact", [128, 8, 2])
        hT = sb("hT", [128, 8, T]); B_hT = [Buf("hT%d" % j) for j in range(8)]
        uT = sb("uT", [128, 8, T], BF16); B_uT = [Buf("uT%d" % j) for j in range(8)]
        yT = sb("yT", [128, 16, T], BF16); B_yT = [Buf("yT%d" % j) for j in range(16)]
        mT = sb("mT", [128, 8, T]); B_mT = [Buf("mT%d" % j) for j in range(8)]
        xin = mT[:].rearrange("p a b -> p (a b)").rearrange("p (c d) -> p c d", c=4)
        B_xin = [[B_mT[2 * c], B_mT[2 * c + 1]] for c in range(4)]
        mTb = sb("mTb", [128, 8, T], BF16); B_mTb = [Buf("mTb%d" % j) for j in range(8)]
        NS = 4
        wsl = [sb("wsl%d" % i, [128, 4096], BF16) for i in range(NS)]
        B_wsl = [Buf("wsl%d" % i) for i in range(NS)]
        wdt = sb("wdt", [128, 256], BF16); B_wdt = Buf("wdt")
        rstd = sb("rstd", [128, T]); B_rstd = Buf("rstd")
        acc = sb("acc", [128, T]); B_acc = Buf("acc")
        sq = [sb("sq%d" % i, [128, T]) for i in range(2)]; B_sq = [Buf("sq0"), Buf("sq1")]
        tmpn = [sb("tmpn%d" % i, [128, T]) for i in range(2)]; B_tmpn = [Buf("tn0"), Buf("tn1")]
        PT = sb("PT", [128, 4, 512]); B_PT = [Buf("PT%d" % g) for g in range(4)]
        PTb = sb("PTb", [128, 4, 512], BF16); B_PTb = [Buf("PTb%d" % g) for g in range(4)]
        Sf = sb("Sf", [128, 8, 256]); B_Sf = [Buf("Sf%d" % h) for h in range(8)]
        Sb = sb("Sb", [128, 8, 256], BF16); B_Sb = [Buf("Sb%d" % h) for h in range(8)]
        hist = sb("hist", [128, 24, 3]); B_hist = [Buf("hist%d" % j) for j in range(24)]
        dtt = sb("dtt", [128, 4, 32]); acs = sb("acs", [128, 4, 32]); nacs = sb("nacs", [128, 4, 32])
        ecs = sb("ecs", [128, 4, 32]); dtw = sb("dtw", [128, 4, 32]); cdec = sb("cdec", [128, 4, 32])
        aa = sb("aa", [128, 4, 32]); tsm = sb("tsm", [128, 4, 32])
        B_dts = Buf("dts")
        posf = sb("posf", [128, NCH]); posi = sb("posi", [128, NCH], I32)
        cs = sb("cs", [128, 2, 4, 64]); B_cs = Buf("cs")
        ang = sb("ang", [128, 4, 64]); angk = sb("angk", [128, 4, 64]); angi = sb("angi", [128, 4, 64], I32)
        angm = sb("angm", [128, 4, 64])
        B_ang = Buf("ang")
        ARENA = Buf("arena")
        psb = [st.enter_context(nc.psum_tensor("ps%d" % i, [128, 512], F32)) for i in range(8)]
        B_ps = [Buf("ps%d" % i) for i in range(8)]
        rr = {"ps": 0, "w": 0, "ev": 0}

        def bank():
            i = rr["ps"]; rr["ps"] = (i + 1) % 8
            return psb[i], B_ps[i]

        def wload(src_ap, n, srcB, view=None):
            i = rr["w"]; rr["w"] = (i + 1) % NS
            dst = wsl[i][:, 0:n]
            if view is not None:
                dst = dst.rearrange(view[0], **view[1])
            S_.dma("w%d" % i, [(dst, src_ap)], reads=[srcB], writes=[B_wsl[i]])
            return wsl[i], B_wsl[i]

        def mm(out, lhsT, rhs, start, stop, reads, writes, inc=None):
            if inc is None:
                inc = stop
            S_.op("pe", lambda e: e.matmul(out, lhsT=lhsT, rhs=rhs, start=start, stop=stop),
                  reads=reads, writes=writes, inc=inc)

        def tr(out, in_, reads, writes, inc=True):
            S_.op("pe", lambda e: e.transpose(out, in_, ident), reads=list(reads) + [B_const],
                  writes=writes, inc=inc)

        def act(out, in_, func, reads, writes, scale=None, bias=None, accum_out=None, eng="act"):
            kw = {}
            if scale is not None:
                kw["scale"] = scale
            if bias is not None:
                kw["bias"] = bias
            if accum_out is not None:
                kw["accum_out"] = accum_out
            if func == AF.Copy and scale is not None and not isinstance(scale, float):
                func = AF.Identity
            S_.op("act", lambda e: e.activation(out=out, in_=in_, func=func, **kw), reads=reads, writes=writes)

        def tt(eng, out, in0, in1, op, reads, writes):
            S_.op(eng, lambda e: e.tensor_tensor(out=out, in0=in0, in1=in1, op=op), reads=reads, writes=writes)

        def ts(eng, out, in0, s1, op0, reads, writes, s2=None, op1=None):
            if op1 is None:
                S_.op(eng, lambda e: e.tensor_scalar(out=out, in0=in0, scalar1=s1, scalar2=None, op0=op0),
                      reads=reads, writes=writes)
            else:
                S_.op(eng, lambda e: e.tensor_scalar(out=out, in0=in0, scalar1=s1, scalar2=s2, op0=op0, op1=op1),
                      reads=reads, writes=writes)

        def stt(eng, out, in0, scalar, in1, op0, op1, reads, writes):
            S_.op(eng, lambda e: e.scalar_tensor_tensor(out=out, in0=in0, scalar=scalar, in1=in1, op0=op0, op1=op1),
                  reads=reads, writes=writes)

        def cp(eng, out, in_, reads, writes):
            if eng == "act":
                act(out, in_, AF.Copy, reads, writes)
            else:
                S_.op(eng, lambda e: e.tensor_copy(out=out, in_=in_), reads=reads, writes=writes)

        def evq():
            rr["ev"] ^= 1
            return "act" if rr["ev"] else "dve"

        def dump(name, ap, bufs, shape):
            if name not in dbg:
                return
            d = nc.dram_tensor("dbg_" + name, list(shape), ap.dtype, kind="ExternalOutput").ap()
            Bd = Buf("dbg")
            S_.dma("dbg", [(d, ap)], reads=bufs, writes=[Bd])
            dbg_out[name] = Bd

        S_.dma("c", [(cf[:], cf_d), (cbt[:], cb_d), (sm[:], sm_d), (posi[:], pos_d)], writes=[B_const])
        S_.op("dve", lambda e: e.tensor_copy(out=posf[:], in_=posi[:]), reads=[B_const], writes=[B_const])
        S_.op("pool", lambda e: e.memset(PT[:], 0.0), writes=B_PT)
        S_.op("pool", lambda e: e.memset(PTb[:], 0.0), writes=B_PTb)
        S_.op("pool", lambda e: e.memset(Sf[:], 0.0), writes=B_Sf)
        S_.op("pool", lambda e: e.memset(Sb[:], 0.0), writes=B_Sb)
        S_.op("pool", lambda e: e.memset(hist[:], 0.0), writes=B_hist)
        act(cact[:, :, 0], cvec, AF.Silu, [B_const], [B_mod])
        act(cact[:, :, 1], cvec, AF.Silu, [B_const], [B_mod])
        act(Aneg[:], hv[:, 32:64], AF.Exp, [B_const], [B_vecs])
        act(Aneg[:], Aneg[:], AF.Copy, [B_vecs], [B_vecs], scale=-1.0)

        with ExitStack() as ph:
            stg = [ph.enter_context(nc.sbuf_tensor(un("adw%d" % i), [128, 8, 512], F32)) for i in range(2)]
            B_stg = [Buf("adw0"), Buf("adw1")]
            pm, Bpm = bank()
            for pc in range(18):
                s_ = stg[pc % 2]; Bs = B_stg[pc % 2]
                S_.dma("pa%d" % (pc % 2), [(s_[:], ada_w[:, pc * 512:(pc + 1) * 512].rearrange("(k p) c -> p k c", p=128))],
                       writes=[Bs])
                for u in range(4):
                    i = pc * 4 + u
                    for k in range(8):
                        mm(pm[:, 2 * i:2 * i + 2], s_[:, k, u * 128:(u + 1) * 128], cact[:, k, :],
                           k == 0, k == 7, [Bs, B_mod], [Bpm], inc=(k == 7))
            S_.op("dve", lambda e: e.tensor_tensor(out=mod[:], in0=pm[:, 0:144].rearrange("p (i t) -> p i t", t=2)[:, :, 0],
                                                   in1=adab, op=ALU.add), reads=[Bpm, B_const], writes=[B_mod])
        for i, (nwo, sco) in enumerate([(0, 8), (8, 32), (16, 56)]):
            stt("dve", vecs[:, 8 * i:8 * i + 8], mod[:, sco:sco + 8], 1.0, nw[:, nwo:nwo + 8], ALU.add, ALU.mult,
                [B_mod, B_const], [B_vecs])
        ts("dve", vecs[:, 24:32], mod[:, 16:24], 0.5, ALU.mult, [B_mod], [B_vecs])
        cp("dve", vecs[:, 32:40], mod[:, 40:48], [B_mod], [B_vecs])
        ts("dve", vecs[:, 40:48], mod[:, 64:72], 0.5, ALU.mult, [B_mod], [B_vecs])
        WM = [vecs[:, 0:8], vecs[:, 8:16], vecs[:, 16:24]]
        SH = [mod[:, 0:8], mod[:, 24:32], mod[:, 48:56]]
        G1H, G2, G3H = vecs[:, 24:32], vecs[:, 32:40], vecs[:, 40:48]
        dump("mod", mod[:], [B_mod], [128, 72])

        S_.wait_all("sp", [B_mod, B_vecs])
        cvq = {"i": 0}

        def cast(out, in_, reads, writes, scale=None):
            e = ("dve", "act", "pool")[cvq["i"] % 3]; cvq["i"] += 1
            if scale is None:
                cp(e, out, in_, reads, writes)
            elif e == "act":
                act(out, in_, AF.Copy, reads, writes, scale=scale)
            else:
                ts(e, out, in_, scale, ALU.mult, reads, writes)

        with ExitStack() as ph:
            stf = [ph.enter_context(nc.sbuf_tensor(un("stf%d" % i), [128, 4096], F32)) for i in range(2)]
            stb = [ph.enter_context(nc.sbuf_tensor(un("stb%d" % i), [128, 4096], BF16)) for i in range(2)]
            B_stf = [Buf("stf0"), Buf("stf1")]; B_stb = [Buf("stb0"), Buf("stb1")]
            pcn = {"i": 0}

            def conv_k1024(src, segs, dst_ap, dstB):
                i = pcn["i"] % 2; pcn["i"] += 1
                W = sum(n for _, n in segs)
                f = stf[i][:, 0:8 * W].rearrange("p (k w) -> p k w", k=8)
                pairs = []; c0 = 0
                for (s0, n) in segs:
                    pairs.append((f[:, :, c0:c0 + n], src[:, s0:s0 + n].rearrange("(k p) c -> p k c", p=128)))
                    c0 += n
                S_.dma("pl%d" % i, pairs, writes=[B_stf[i]])
                cast(stb[i][:, 0:8 * W], stf[i][:, 0:8 * W], [B_stf[i]], [B_stb[i]])
                S_.dma("ps%d" % i, [(dst_ap, stb[i][:, 0:8 * W])], reads=[B_stb[i]], writes=[dstB], eng="act")

            def conv_rows(src, nk, dst, dstB, scale_ap=None):
                for k0 in range(0, nk, 4):
                    kk = min(4, nk - k0)
                    i = pcn["i"] % 2; pcn["i"] += 1
                    f = stf[i][:, 0:kk * 1024].rearrange("p (k c) -> p k c", k=kk)
                    S_.dma("pl%d" % i, [(f, src[k0 * 128:(k0 + kk) * 128, :].rearrange("(k p) c -> p k c", p=128))],
                           writes=[B_stf[i]])
                    bv = stb[i][:, 0:kk * 1024].rearrange("p (m k c) -> p m k c", m=8, k=kk)
                    for k in range(kk):
                        cast(bv[:, :, k, :], f[:, k, :].rearrange("p (m c) -> p m c", m=8), [B_stf[i]], [B_stb[i]],
                             scale=None if scale_ap is None else scale_ap[:, k0 + k:k0 + k + 1])
                    S_.dma("ps%d" % i, [(dst[:, :, k0 * 128:(k0 + kk) * 128].rearrange("m p f -> p m f"),
                                   stb[i][:, 0:kk * 1024].rearrange("p (m f) -> p m f", m=8))],
                           reads=[B_stb[i]], writes=[dstB], eng="act")

            def cv_ffn(wi, wo, s_i, s_o, nm):
                scrB[nm + "i"] = Buf(nm + "i"); scrB[nm + "o"] = Buf(nm + "o")
                for jj in range(11):
                    conv_k1024(wi, [(256 * jj, 256), (FF + 256 * jj, 256)], s_i[jj], scrB[nm + "i"])
                conv_rows(wo, 22, s_o, scrB[nm + "o"])

            cv_ffn(f1i, f1o, s_f1i, s_f1o, "f1")
            scrB["mix"] = Buf("mix")
            for g in range(4):
                conv_k1024(mwi, [(O_Z + 512 * g, 512)], s_z[g], scrB["mix"])
                conv_k1024(mwi, [(O_XS + 512 * g, 512)], s_xs[g], scrB["mix"])
                conv_k1024(mwi, [(O_B + 128 * g, 128), (O_C + 128 * g, 128)], s_bc[g], scrB["mix"])
            conv_k1024(mwi, [(O_DT, 32)], s_dt, scrB["mix"])
            for h in range(8):
                conv_k1024(mwi, [(O_Q + 128 * h, 128), (O_K + 128 * h, 128)], s_qk[h], scrB["mix"])
                conv_k1024(mwi, [(O_V + 256 * h, 256), (O_G + 256 * h, 256)], s_vg[h], scrB["mix"])
            for b2 in range(2):
                conv_k1024(mwi, [(O_GS + 512 * b2, 512)], s_gs[b2], scrB["mix"])
                conv_k1024(mwi, [(O_GR + 512 * b2, 512)], s_gr[b2], scrB["mix"])
                conv_k1024(wo_d, [(512 * b2, 512)], s_wo[b2], scrB["mix"])
            conv_rows(wbs_d, 16, s_bs, scrB["mix"], scale_ap=snw)
            conv_rows(wbr_d, 16, s_br, scrB["mix"], scale_ap=rnw)
            cv_ffn(f2i, f2o, s_f2i, s_f2o, "f2")
        for nm_ in ("ps0", "ps1"):
            st_ = S_.stream(nm_)
            nc.sync.wait_ge(st_[0], st_[1])
        S_.dma("wd", [(wdt[:], s_dt)], reads=[scrB["mix"]], writes=[B_wdt])

        def norm_to_u(idx, final_out=None):
            for j in range(8):
                if j == 0:
                    act(acc[:], hT[:, 0, :], AF.Square, [B_hT[0]], [B_acc])
                else:
                    b = j % 2
                    act(sq[b][:], hT[:, j, :], AF.Square, [B_hT[j]], [B_sq[b]])
                    tt("pool", acc[:], acc[:], sq[b][:], ALU.add, [B_acc, B_sq[b]], [B_acc])
            pn, Bpn = bank()
            mm(pn[:], ones, acc[:], True, True, [B_acc, B_const], [Bpn])
            act(rstd[:], pn[:], AF.Sqrt, [Bpn], [B_rstd], scale=1.0 / D, bias=EPS)
            S_.op("dve", lambda e: e.reciprocal(out=rstd[:], in_=rstd[:]), reads=[B_rstd], writes=[B_rstd])
            for j in range(8):
                b = j % 2
                if final_out is None:
                    tt("dve", tmpn[b][:], hT[:, j, :], rstd[:], ALU.mult, [B_hT[j], B_rstd], [B_tmpn[b]])
                    act(uT[:, j, :], tmpn[b][:], AF.Identity, [B_tmpn[b], B_vecs, B_mod], [B_uT[j]],
                        scale=WM[idx][:, j:j + 1], bias=SH[idx][:, j:j + 1])
                else:
                    fo, Bfo = final_out
                    stt("dve", fo[:, j, :], hT[:, j, :], nw[:, 24 + j:25 + j], rstd[:], ALU.mult, ALU.mult,
                        [B_hT[j], B_rstd, B_const], [Bfo[j]])

        def ffn(s_i, s_o, Bsi, Bso, gh, ph):
            hid = ph.enter_context(nc.sbuf_tensor(un("hid"), [128, 22, T], BF16))
            B_hid = [Buf("hid%d" % j, ARENA) for j in range(22)]
            sg = [ph.enter_context(nc.sbuf_tensor(un("sg%d" % i), [128, T], F32)) for i in range(2)]
            B_sg = [Buf("sg0", ARENA), Buf("sg1", ARENA)]
            for jj in range(11):
                w, Bw = wload(s_i[jj], 4096, Bsi)
                for u in range(2):
                    j = 2 * jj + u
                    pa, Bpa = bank(); pb, Bpb = bank()
                    for k in range(8):
                        mm(pa[:], w[:, k * 512 + u * 128:k * 512 + (u + 1) * 128], uT[:, k, :], k == 0, k == 7,
                           [Bw, B_uT[k]], [Bpa])
                    for k in range(8):
                        mm(pb[:], w[:, k * 512 + 256 + u * 128:k * 512 + 256 + (u + 1) * 128], uT[:, k, :], k == 0, k == 7,
                           [Bw, B_uT[k]], [Bpb])
                    b = j % 2
                    act(sg[b][:], pa[:], AF.Silu, [Bpa], [B_sg[b]])
                    tt("dve", hid[:, j, :], sg[b][:], pb[:], ALU.mult, [B_sg[b], Bpb], [B_hid[j]])
            for m in range(8):
                w, Bw = wload(s_o[m], FF, Bso)
                po, Bpo = bank()
                for k in range(22):
                    mm(po[:], w[:, k * 128:(k + 1) * 128], hid[:, k, :], k == 0, k == 21, [Bw, B_hid[k]], [Bpo])
                stt("dve", hT[:, m, :], po[:], gh[:, m:m + 1], hT[:, m, :], ALU.mult, ALU.add,
                    [Bpo, B_vecs, B_hT[m]], [B_hT[m]])

        def phase_switch():
            S_.op("pool", lambda e: e.memset(angm[:, 0, 0:1], 0.0), reads=[], writes=[ARENA, B_ang])

        def branch_out(s_w, s_g, gb_off, first):
            gw = [None, None]
            for m in range(8):
                if m % 2 == 0:
                    wA, BwA = wload(s_w[m:m + 2].rearrange("m p f -> p m f"), 4096, scrB["mix"], view=("p (m f) -> p m f", {"m": 2}))
                if m % 4 == 0:
                    gw = wload(s_g[m // 4], 4096, scrB["mix"])
                wG, BwG = gw
                pa, Bpa = bank(); pb, Bpb = bank()
                for k in range(16):
                    mm(pa[:], wA[:, (m % 2) * 2048 + k * 128:(m % 2) * 2048 + (k + 1) * 128], yT[:, k, :], k == 0, k == 15,
                       [BwA, B_yT[k]], [Bpa])
                for k in range(8):
                    u = m % 4
                    mm(pb[:], wG[:, k * 512 + u * 128:k * 512 + (u + 1) * 128], uT[:, k, :], k == 0, k == 7,
                       [BwG, B_uT[k]], [Bpb])
                b = m % 2
                act(tmpn[b][:], pb[:], AF.Sigmoid, [Bpb, B_const], [B_tmpn[b]], bias=gateb[:, gb_off + m:gb_off + m + 1])
                if first:
                    tt("dve", mT[:, m, :], tmpn[b][:], pa[:], ALU.mult, [B_tmpn[b], Bpa], [B_mT[m]])
                else:
                    tt("dve", tmpn[b][:], tmpn[b][:], pa[:], ALU.mult, [B_tmpn[b], Bpa], [B_tmpn[b]])
                    tt("pool", mTb[:, m, :], tmpn[b][:], mT[:, m, :], ALU.add, [B_tmpn[b], B_mT[m]], [B_mTb[m]])

        def rms_scale(ssum, n, r, Bs):
            act(r, ssum, AF.Sqrt, [Bs], [Bs], scale=1.0 / n, bias=EPS)
            S_.op("dve", lambda e: e.reciprocal(out=r, in_=r), reads=[Bs], writes=[Bs])

        for t in range(NT):
            tok0 = t * T
            S_.dma("x", [(xin, x_d[tok0:tok0 + T, :].rearrange("(c p) d -> p c d", p=128))], writes=B_mT)
            for j in range(8):
                pt, Bpt = bank()
                for c in range(4):
                    tr(pt[:, c * 128:(c + 1) * 128], xin[:, c, j * 128:(j + 1) * 128], B_xin[c], [Bpt], inc=(c == 3))
                cp(evq(), hT[:, j, :], pt[:], [Bpt], [B_hT[j]])

            norm_to_u(0)
            with ExitStack() as ph:
                ffn(s_f1i, s_f1o, scrB["f1i"], scrB["f1o"], G1H, ph)
            if t == 0:
                dump("h1", hT[:], B_hT, [128, 8, T])
            phase_switch()

            norm_to_u(1)
            if t == 0:
                dump("u2", uT[:], B_uT, [128, 8, T])
            pd, Bpd = bank()
            for c in range(4):
                for k in range(8):
                    mm(pd[:, c * 32:(c + 1) * 32], uT[:, k, c * 128:(c + 1) * 128], wdt[:, k * 32:(k + 1) * 32],
                       k == 0, k == 7, [B_uT[k], B_wdt], [Bpd], inc=(k == 7))
            pd3 = pd[:, 0:128].rearrange("p (c h) -> p c h", c=4)
            bc3 = lambda ap: ap.unsqueeze(1).broadcast_to([128, 4, 32])
            tt("dve", tsm[:], pd3, bc3(hv[:, 0:32]), ALU.add, [Bpd, B_const], [B_dts])
            act(tsm[:], tsm[:], AF.Exp, [B_dts], [B_dts])
            act(dtt[:], tsm[:], AF.Ln, [B_dts], [B_dts], bias=1.0)
            tt("dve", aa[:], dtt[:], bc3(Aneg[:]), ALU.mult, [B_dts, B_vecs], [B_dts])
            pc_, Bpc = bank()
            for c in range(4):
                mm(pc_[:, c * 32:(c + 1) * 32], Umat, aa[:, c, :], True, True, [B_const, B_dts], [Bpc], inc=(c == 3))
            for c in range(4):
                mm(pc_[:, 128 + c * 32:128 + (c + 1) * 32], ones, aa[:, c, :], True, True, [B_const, B_dts], [Bpc],
                   inc=(c == 3))
            pc3 = pc_[:, 0:128].rearrange("p (c h) -> p c h", c=4)
            pe3 = pc_[:, 128:256].rearrange("p (c h) -> p c h", c=4)
            act(acs[:], pc3, AF.Copy, [Bpc], [B_dts])
            act(nacs[:], pc3, AF.Copy, [Bpc], [B_dts], scale=-1.0)
            act(ecs[:], pc3, AF.Exp, [Bpc], [B_dts])
            act(cdec[:], pe3, AF.Exp, [Bpc], [B_dts])
            tt("dve", tsm[:], pe3, acs[:], ALU.subtract, [Bpc, B_dts], [B_dts])
            act(tsm[:], tsm[:], AF.Exp, [B_dts], [B_dts])
            tt("dve", dtw[:], tsm[:], dtt[:], ALU.mult, [B_dts], [B_dts])
            if t == 0:
                dump("dtt", dtt[:], [B_dts], [128, 4, 32])
                dump("acs", acs[:], [B_dts], [128, 4, 32])

            with ExitStack() as ph:
                def asb(name, shape, dt=F32):
                    return ph.enter_context(nc.sbuf_tensor(un(name), list(shape), dt))
                szb = asb("szb", [128, 4, 512], BF16); B_szb = [Buf("szb%d" % c, ARENA) for c in range(4)]
                raw = [asb("raw%d" % i, [128, 515]) for i in range(2)]; B_raw = [Buf("raw%d" % i, ARENA) for i in range(2)]
                cacc = [asb("cacc%d" % i, [128, 512]) for i in range(2)]; B_cacc = [Buf("cacc%d" % i, ARENA) for i in range(2)]
                xsT = asb("xsT", [128, 4, 512]); B_xsT = [Buf("xsT%d" % u, ARENA) for u in range(4)]
                Bf = asb("Bf", [128, 512]); B_Bf = Buf("Bf", ARENA)
                BTb = asb("BTb", [128, 512], BF16); CTb = asb("CTb", [128, 512], BF16)
                B_BTb = Buf("BTb", ARENA); B_CTb = Buf("CTb", ARENA)
                Xdt = [asb("Xdt%d" % i, [128, 512], BF16) for i in range(1)] * 2
                Xw = [asb("Xw%d" % i, [128, 512], BF16) for i in range(1)] * 2
                XD = [asb("XD%d" % i, [128, 512]) for i in range(1)] * 2
                B_X = [Buf("X%d" % i, ARENA) for i in range(1)] * 2
                Btok = [asb("Btok%d" % i, [128, 128], BF16) for i in range(2)]; B_Btok = [Buf("Btok%d" % i, ARENA) for i in range(2)]
                cbT = [asb("cbT%d" % i, [128, 128]) for i in range(2)]; B_cbT = [Buf("cbT%d" % i, ARENA) for i in range(2)]
                AU = asb("AU", [128, 8, 128]); B_AU = Buf("AU", ARENA)
                Eb = [asb("Eb%d" % i, [128, 4, 128]) for i in range(2)]; B_Eb = [Buf("Eb%d" % i, ARENA) for i in range(2)]
                scT = [asb("scT%d" % i, [128, 8, 128], BF16) for i in range(2)]; B_scT = [Buf("scT%d" % i, ARENA) for i in range(2)]
                yb = [asb("yb%d" % i, [128, 512]) for i in range(1)] * 2; B_yb = [Buf("yb%d" % i, ARENA) for i in range(1)] * 2
                ygn = [asb("ygn%d" % i, [128, 512]) for i in range(1)] * 2; B_ygn = [Buf("ygn%d" % i, ARENA) for i in range(1)] * 2
                ssq = asb("ssq", [128, 8]); B_ssq = [Buf("ssq%d" % i, ARENA) for i in range(2)]
                for g in range(4):
                    wz, Bwz = wload(s_z[g], 4096, scrB["mix"])
                    wx, Bwx = wload(s_xs[g], 4096, scrB["mix"])
                    wbc, Bwbc = wload(s_bc[g], 2048, scrB["mix"])
                    for c in range(4):
                        pz, Bpz = bank()
                        for k in range(8):
                            mm(pz[:], uT[:, k, c * 128:(c + 1) * 128], wz[:, k * 512:(k + 1) * 512], k == 0, k == 7,
                               [B_uT[k], Bwz], [Bpz])
                        act(szb[:, c, :], pz[:], AF.Silu, [Bpz], [B_szb[c]])
                    for u in range(6):
                        px, Bpx = bank()
                        for k in range(8):
                            if u < 4:
                                lw = wx[:, k * 512 + u * 128:k * 512 + (u + 1) * 128]; Bl = Bwx
                            else:
                                lw = wbc[:, k * 256 + (u - 4) * 128:k * 256 + (u - 3) * 128]; Bl = Bwbc
                            mm(px[:], lw, uT[:, k, :], k == 0, k == 7, [Bl, B_uT[k]], [Bpx])
                        jch = (4 * g + u) if u < 4 else (16 + g if u == 4 else 20 + g)
                        b = u % 2
                        cp("pool", raw[b][:, 0:3], hist[:, jch, :], [B_hist[jch]], [B_raw[b]])
                        cp("act", raw[b][:, 3:515], px[:], [Bpx], [B_raw[b]])
                        cp("pool", hist[:, jch, :], raw[b][:, 512:515], [B_raw[b]], [B_hist[jch]])
                        e1 = "dve"
                        ts(e1, cacc[b][:], raw[b][:, 0:512], convw[:, 4 * jch:4 * jch + 1], ALU.mult, [B_raw[b], B_const], [B_cacc[b]])
                        for tap in range(1, 4):
                            stt(e1, cacc[b][:], raw[b][:, tap:tap + 512], convw[:, 4 * jch + tap:4 * jch + tap + 1], cacc[b][:],
                                ALU.mult, ALU.add, [B_raw[b], B_const, B_cacc[b]], [B_cacc[b]])
                        cb_ = convb[:, jch:jch + 1]
                        if u < 4:
                            act(xsT[:, u, :], cacc[b][:], AF.Silu, [B_cacc[b], B_const], [B_xsT[u]], bias=cb_)
                        elif u == 4:
                            act(Bf[:], cacc[b][:], AF.Silu, [B_cacc[b], B_const], [B_Bf], bias=cb_)
                            cp("pool", BTb[:], Bf[:], [B_Bf], [B_BTb])
                        else:
                            act(CTb[:], cacc[b][:], AF.Silu, [B_cacc[b], B_const], [B_CTb], bias=cb_)
                    if t == 0 and g == 0:
                        dump("xsT", xsT[:], B_xsT, [128, 4, 512])
                        dump("CTb", CTb[:], [B_CTb], [128, 512])
                    hs = slice(8 * g, 8 * g + 8)
                    for c in range(4):
                        b = c % 2
                        cs_ = slice(c * 128, (c + 1) * 128)
                        bh = lambda ap: ap.unsqueeze(2).broadcast_to([128, 8, 64])
                        v3 = lambda ap: ap.rearrange("p (h e) -> p h e", h=8)
                        px, Bpx = bank()
                        for u in range(4):
                            tr(px[:, u * 128:(u + 1) * 128], xsT[:, u, cs_], [B_xsT[u]], [Bpx], inc=(u == 3))
                        tt("dve", v3(Xdt[b][:]), v3(px[:]), bh(dtt[:, c, hs]), ALU.mult, [Bpx, B_dts], [B_X[b]])
                        tt("dve", v3(Xw[b][:]), v3(px[:]), bh(dtw[:, c, hs]), ALU.mult, [Bpx, B_dts], [B_X[b]])
                        tt("dve", v3(XD[b][:]), v3(px[:]), bh(hv[:, 64 + 8 * g:72 + 8 * g]), ALU.mult, [Bpx, B_const], [B_X[b]])
                        pbt, Bpbt = bank()
                        tr(pbt[:, 0:128], Bf[:, cs_], [B_Bf], [Bpbt])
                        cp("act", Btok[b][:], pbt[:, 0:128], [Bpbt], [B_Btok[b]])
                        pcb, Bpcb = bank()
                        mm(pcb[:, 0:128], BTb[:, cs_], CTb[:, cs_], True, True, [B_BTb, B_CTb], [Bpcb])
                        cp("act", cbT[b][:], pcb[:, 0:128], [Bpcb], [B_cbT[b]])
                        tt("pool", AU[:], aa[:, c, hs].unsqueeze(2).broadcast_to([128, 8, 128]),
                           Umat.unsqueeze(1).broadcast_to([128, 8, 128]), ALU.mult, [B_dts, B_const], [B_AU])
                        for q4 in range(2):
                            pbc, Bpbc = bank()
                            mm(pbc[:], ones, AU[:, 4 * q4:4 * q4 + 4, :].rearrange("p h l -> p (h l)"), True, False,
                               [B_const, B_AU], [Bpbc], inc=False)
                            mm(pbc[:], identb, negm4, False, True, [B_const], [Bpbc])
                            eb = Eb[q4]; Beb = B_Eb[q4]
                            for hh in range(4):
                                h = 8 * g + 4 * q4 + hh
                                act(eb[:, hh, :], pbc[:, hh * 128:(hh + 1) * 128], AF.Exp, [Bpbc, B_dts], [Beb],
                                    bias=nacs[:, c, h:h + 1])
                            tt("dve", scT[b][:, 4 * q4:4 * q4 + 4, :], eb[:], cbT[b][:].unsqueeze(1).broadcast_to([128, 4, 128]),
                               ALU.mult, [Beb, B_cbT[b]], [B_scT[b]])
                        py, Bpy = bank()
                        for hh in range(8):
                            mm(py[:, hh * 64:(hh + 1) * 64], scT[b][:, hh, :], Xdt[b][:, hh * 64:(hh + 1) * 64], True, True,
                               [B_scT[b], B_X[b]], [Bpy], inc=(hh == 7))
                        po, Bpo = bank()
                        mm(po[:], CTb[:, cs_], PTb[:, g, :], True, True, [B_CTb, B_PTb[g]], [Bpo])
                        tt("dve", v3(yb[b][:]), v3(po[:]), bh(ecs[:, c, hs]), ALU.mult, [Bpo, B_dts], [B_yb[b]])
                        tt("dve", yb[b][:], yb[b][:], py[:], ALU.add, [B_yb[b], Bpy], [B_yb[b]])
                        tt("pool", yb[b][:], yb[b][:], XD[b][:], ALU.add, [B_yb[b], B_X[b]], [B_yb[b]])
                        if t == 0 and g == 0 and c == 1:
                            dump("y01", yb[b][:], [B_yb[b]], [128, 512])
                        tt("pool", yb[b][:], yb[b][:], szb[:, c, :], ALU.mult, [B_yb[b], B_szb[c]], [B_yb[b]])
                        act(ygn[b][:], yb[b][:], AF.Square, [B_yb[b]], [B_ygn[b], B_ssq[b]], accum_out=ssq[:, 2 * b:2 * b + 1])
                        rms_scale(ssq[:, 2 * b:2 * b + 1], 512, ssq[:, 2 * b + 1:2 * b + 2], B_ssq[b])
                        act(ygn[b][:], yb[b][:], AF.Copy, [B_yb[b], B_ssq[b]], [B_ygn[b]], scale=ssq[:, 2 * b + 1:2 * b + 2])
                        pyt, Bpyt = bank()
                        for u in range(4):
                            tr(pyt[:, u * 128:(u + 1) * 128], ygn[b][:, u * 128:(u + 1) * 128], [B_ygn[b]], [Bpyt], inc=(u == 3))
                        cp(evq(), yT[:, 4 * g:4 * g + 4, cs_], pyt[:].rearrange("p (u t) -> p u t", u=4), [Bpyt],
                           B_yT[4 * g:4 * g + 4])
                        pst, Bpst = bank()
                        mm(pst[:], Btok[b][:], Xw[b][:], True, True, [B_Btok[b], B_X[b]], [Bpst])
                        tt("pool", v3(PT[:, g, :]), v3(PT[:, g, :]), bh(cdec[:, c, hs]), ALU.mult, [B_PT[g], B_dts], [B_PT[g]])
                        tt("dve", PT[:, g, :], PT[:, g, :], pst[:], ALU.add, [B_PT[g], Bpst], [B_PT[g]])
                        cp("pool", PTb[:, g, :], PT[:, g, :], [B_PT[g]], [B_PTb[g]])
            if t == 0:
                dump("ygT", yT[:], B_yT, [128, 16, T])
            branch_out(s_bs, s_gs, 0, True)
            if t == 0:
                dump("mT", mT[:], B_mT, [128, 8, T])
            phase_switch()

            for c in range(4):
                ch = t * 4 + c
                ts("dve", ang[:, c, :], invf, posf[:, ch:ch + 1], ALU.mult, [B_const], [B_ang])
            ts("dve", angk[:], ang[:], 1.0 / TWO_PI, ALU.mult, [B_ang], [B_ang])
            cp("dve", angi[:], angk[:], [B_ang], [B_ang])
            cp("dve", angk[:], angi[:], [B_ang], [B_ang])
            stt("dve", ang[:], angk[:], -CW1, ang[:], ALU.mult, ALU.add, [B_ang], [B_ang])
            stt("dve", ang[:], angk[:], -CW2, ang[:], ALU.mult, ALU.add, [B_ang], [B_ang])

            def wrap(dst):
                ts("dve", angm[:], dst, math.pi, ALU.is_gt, [B_ang], [B_ang], s2=-TWO_PI, op1=ALU.mult)
                tt("dve", dst, dst, angm[:], ALU.add, [B_ang], [B_ang])
                ts("dve", angm[:], dst, -math.pi, ALU.is_lt, [B_ang], [B_ang], s2=TWO_PI, op1=ALU.mult)
                tt("dve", dst, dst, angm[:], ALU.add, [B_ang], [B_ang])
            wrap(ang[:])
            act(cs[:, 1], ang[:], AF.Sin, [B_ang], [B_cs])
            ts("dve", ang[:], ang[:], math.pi / 2, ALU.add, [B_ang], [B_ang])
            wrap(ang[:])
            act(cs[:, 0], ang[:], AF.Sin, [B_ang], [B_cs])
            if t == 0:
                dump("cs", cs[:], [B_cs], [128, 2, 4, 64])

            with ExitStack() as ph:
                def asb(name, shape, dt=F32):
                    return ph.enter_context(nc.sbuf_tensor(un(name), list(shape), dt))
                qk = asb("qk", [128, 4, 256]); B_qk = Buf("qk", ARENA)
                r1 = asb("r1", [128, 4, 2, 64]); r2 = asb("r2", [128, 4, 2, 64]); B_r = Buf("r12", ARENA)
                QKd = asb("QKd", [128, 4, 256]); B_QKd = Buf("QKd", ARENA)
                Kdb = asb("Kdb", [128, 4, 128], BF16); B_Kdb = Buf("Kdb", ARENA)
                vb = asb("vb", [128, 4, 256], BF16); B_vb = Buf("vb", ARENA)
                sgb = asb("sgb", [128, 4, 256], BF16); B_sgb = Buf("sgb", ARENA)
                QT = asb("QT", [128, 512], BF16); KT = asb("KT", [128, 512], BF16)
                B_QT = Buf("QT", ARENA); B_KT = Buf("KT", ARENA)
                scR = asb("scR", [128, 4, 128], BF16); B_scR = Buf("scR", ARENA)
                yrn = [asb("yrn%d" % i, [128, 256]) for i in range(2)]; B_yrn = [Buf("yrn%d" % i, ARENA) for i in range(2)]
                junk = asb("junk", [128, 256]); B_junk = Buf("junk", ARENA)
                rs = asb("rs", [128, 4]); B_rs = [Buf("rs%d" % i, ARENA) for i in range(2)]
                stmp = asb("stmp", [128, 256]); B_stmp = Buf("stmp", ARENA)
                for h in range(8):
                    wqk, Bwqk = wload(s_qk[h], 2048, scrB["mix"])
                    wvg, Bwvg = wload(s_vg[h], 4096, scrB["mix"])
                    for c in range(4):
                        pq, Bpq = bank(); pv, Bpv = bank()
                        for k in range(8):
                            mm(pq[:, 0:256], uT[:, k, c * 128:(c + 1) * 128], wqk[:, k * 256:(k + 1) * 256], k == 0, k == 7,
                               [B_uT[k], Bwqk], [Bpq])
                        for k in range(8):
                            mm(pv[:], uT[:, k, c * 128:(c + 1) * 128], wvg[:, k * 512:(k + 1) * 512], k == 0, k == 7,
                               [B_uT[k], Bwvg], [Bpv])
                        cp("dve", qk[:, c, :], pq[:, 0:256], [Bpq], [B_qk])
                        cp("act", vb[:, c, :], pv[:, 0:256], [Bpv], [B_vb])
                        act(sgb[:, c, :], pv[:, 256:512], AF.Silu, [Bpv], [B_sgb])
                    q5 = qk[:].rearrange("p c (a f i) -> p c a f i", a=2, f=2)
                    o5 = QKd[:].rearrange("p c (a f i) -> p c a f i", a=2, f=2)
                    x1 = q5[:, :, :, 0, :]; x2 = q5[:, :, :, 1, :]
                    cosb = cs[:, 0].unsqueeze(2).broadcast_to([128, 4, 2, 64])
                    sinb = cs[:, 1].unsqueeze(2).broadcast_to([128, 4, 2, 64])
                    tt("dve", r1[:], x1, cosb, ALU.mult, [B_qk, B_cs], [B_r])
                    tt("pool", r2[:], x2, sinb, ALU.mult, [B_qk, B_cs], [B_r])
                    tt("dve", o5[:, :, :, 0, :], r1[:], r2[:], ALU.subtract, [B_r], [B_QKd])
                    tt("dve", r1[:], x2, cosb, ALU.mult, [B_qk, B_cs], [B_r])
                    tt("pool", r2[:], x1, sinb, ALU.mult, [B_qk, B_cs], [B_r])
                    tt("dve", o5[:, :, :, 1, :], r1[:], r2[:], ALU.add, [B_r], [B_QKd])
                    ts("dve", QKd[:, :, 0:128], QKd[:, :, 0:128], qdec[:, h:h + 1], ALU.mult, [B_QKd, B_const], [B_QKd])
                    ts("pool", QKd[:, :, 128:256], QKd[:, :, 128:256], kdec[:, h:h + 1], ALU.mult, [B_QKd, B_const], [B_QKd])
                    cp("pool", Kdb[:], QKd[:, :, 128:256], [B_QKd], [B_Kdb])
                    if t == 0 and h == 0:
                        dump("QKd", QKd[:], [B_QKd], [128, 4, 256])
                    pqt, Bpqt = bank(); pkt, Bpkt = bank()
                    for c in range(4):
                        tr(pqt[:, c * 128:(c + 1) * 128], QKd[:, c, 0:128], [B_QKd], [Bpqt], inc=(c == 3))
                    for c in range(4):
                        tr(pkt[:, c * 128:(c + 1) * 128], QKd[:, c, 128:256], [B_QKd], [Bpkt], inc=(c == 3))
                    cp("act", QT[:], pqt[:], [Bpqt], [B_QT])
                    cp("dve", KT[:], pkt[:], [Bpkt], [B_KT])
                    psc, Bpsc = bank()
                    for c in range(4):
                        cs_ = slice(c * 128, (c + 1) * 128)
                        mm(psc[:, cs_], KT[:, cs_], QT[:, cs_], True, True, [B_KT, B_QT], [Bpsc], inc=(c == 3))
                    tt("dve", scR[:], psc[:].rearrange("p (c l) -> p c l", c=4), rmask.unsqueeze(1).broadcast_to([128, 4, 128]),
                       ALU.mult, [Bpsc, B_const], [B_scR])
                    for c in range(4):
                        b = c % 2
                        cs_ = slice(c * 128, (c + 1) * 128)
                        pyr, Bpyr = bank()
                        mm(pyr[:, 0:256], scR[:, c, :], vb[:, c, :], True, False, [B_scR, B_vb], [Bpyr], inc=False)
                        mm(pyr[:, 0:256], QT[:, cs_], Sb[:, h, :], False, True, [B_QT, B_Sb[h]], [Bpyr])
                        act(junk[:], pyr[:, 0:256], AF.Square, [Bpyr], [B_junk, B_rs[b]], accum_out=rs[:, 2 * b:2 * b + 1])
                        rms_scale(rs[:, 2 * b:2 * b + 1], 256, rs[:, 2 * b + 1:2 * b + 2], B_rs[b])
                        stt("dve", yrn[b][:], pyr[:, 0:256], rs[:, 2 * b + 1:2 * b + 2], sgb[:, c, :], ALU.mult, ALU.mult,
                            [Bpyr, B_rs[b], B_sgb], [B_yrn[b]])
                        pyt, Bpyt = bank()
                        for e2 in range(2):
                            tr(pyt[:, e2 * 128:(e2 + 1) * 128], yrn[b][:, e2 * 128:(e2 + 1) * 128], [B_yrn[b]], [Bpyt], inc=(e2 == 1))
                        cp(evq(), yT[:, 2 * h:2 * h + 2, cs_], pyt[:, 0:256].rearrange("p (u t) -> p u t", u=2), [Bpyt],
                           B_yT[2 * h:2 * h + 2])
                        pss, Bpss = bank()
                        mm(pss[:, 0:256], Kdb[:, c, :], vb[:, c, :], True, True, [B_Kdb, B_vb], [Bpss])
                        act(stmp[:], pss[:, 0:256], AF.Copy, [Bpss], [B_stmp], scale=G128[h])
                        stt("dve", Sf[:, h, :], Sf[:, h, :], G128[h], stmp[:], ALU.mult, ALU.add, [B_Sf[h], B_stmp], [B_Sf[h]])
                        cp("pool", Sb[:, h, :], Sf[:, h, :], [B_Sf[h]], [B_Sb[h]])
            if t == 0:
                dump("yrT", yT[:], B_yT, [128, 16, T])
            branch_out(s_br, s_gr, 8, False)
            for m in range(8):
                if m % 4 == 0:
                    wO, BwO = wload(s_wo[m // 4], 4096, scrB["mix"])
                po, Bpo = bank()
                u = m % 4
                for k in range(8):
                    mm(po[:], wO[:, k * 512 + u * 128:k * 512 + (u + 1) * 128], mTb[:, k, :], k == 0, k == 7, [BwO, B_mTb[k]], [Bpo])
                stt("dve", hT[:, m, :], po[:], G2[:, m:m + 1], hT[:, m, :], ALU.mult, ALU.add, [Bpo, B_vecs, B_hT[m]], [B_hT[m]])
            if t == 0:
                dump("h2", hT[:], B_hT, [128, 8, T])
            phase_switch()

            norm_to_u(2)
            with ExitStack() as ph:
                ffn(s_f2i, s_f2o, scrB["f2i"], scrB["f2o"], G3H, ph)
            phase_switch()

            with ExitStack() as ph:
                oT = ph.enter_context(nc.sbuf_tensor(un("oT"), [128, 8, T], F32))
                B_oT = [Buf("oT%d" % j, ARENA) for j in range(8)]
                ost = ph.enter_context(nc.sbuf_tensor(un("ost"), [128, 4, D], F32))
                B_ost = [Buf("ost%d" % c, ARENA) for c in range(4)]
                norm_to_u(3, final_out=(oT, B_oT))
                for c in range(4):
                    for hf in range(2):
                        pt, Bpt = bank()
                        for jj in range(4):
                            j = hf * 4 + jj
                            tr(pt[:, jj * 128:(jj + 1) * 128], oT[:, j, c * 128:(c + 1) * 128], [B_oT[j]], [Bpt], inc=(jj == 3))
                        cp(evq(), ost[:, c, hf * 512:(hf + 1) * 512], pt[:], [Bpt], [B_ost[c]])
                S_.dma("o", [(out_d[tok0:tok0 + T, :].rearrange("(c p) d -> p c d", p=128), ost[:])], reads=B_ost, eng="act")
            phase_switch()

        fin = Buf("fin")
        st_o = S_.stream("o")
        nc.sync.wait_ge(st_o[0], st_o[1])
        for name, Bd in dbg_out.items():
            S_.wait_all("sp", [Bd])
    print("build: ops=%d counts=%s" % (S_.nops, S_.cnt))
    return nc, list(dbg_out.keys())


def _lay(v, n):
    return np.ascontiguousarray(np.asarray(v, np.float32).reshape(n, 128).T)


def make_in_maps(inputs, S, batch_ids):
    cf, cb, _ = host_consts()
    L = 0
    f = lambda k: np.ascontiguousarray(np.asarray(inputs[k], np.float32)[L])
    shared = {
        "cf": cf, "cb": cb,
        "ada_w": f("ada_w"), "ffn1_w_in": f("ffn1_w_in"), "ffn1_w_out": f("ffn1_w_out"),
        "ffn2_w_in": f("ffn2_w_in"), "ffn2_w_out": f("ffn2_w_out"), "mix_w_in": f("mix_w_in"),
        "w_br_ssd": f("w_br_ssd"), "w_br_ret": f("w_br_ret"), "mix_w_out": f("mix_w_out"),
    }
    nwl = np.concatenate([_lay(inputs["norm_ffn1_w"][L], 8), _lay(inputs["norm_mix_w"][L], 8),
                          _lay(inputs["norm_ffn2_w"][L], 8), _lay(inputs["norm_final_w"], 8)], axis=1)
    cw = np.asarray(inputs["ssd_conv_w"], np.float32)[L]
    convw = np.ascontiguousarray(cw.reshape(4, 24, 128).transpose(2, 1, 0)).reshape(128, 96)
    rep = lambda v: np.tile(np.asarray(v, np.float32).reshape(1, -1), (128, 1))
    hv = np.concatenate([rep(inputs["ssd_dt_bias"][L]), rep(inputs["ssd_a_log"][L]), rep(inputs["ssd_d"][L])], axis=1)
    maps = []
    for b in batch_ids:
        smalls = np.concatenate([
            _lay(inputs["c"][b], 8), _lay(inputs["ada_b"][L], 72), nwl, _lay(inputs["mix_gate_b"][L], 16),
            convw, _lay(inputs["ssd_conv_b"][L], 24), hv, _lay(inputs["ssd_norm_w"][L], 16),
            _lay(inputs["ret_norm_w"][L], 16)], axis=1).astype(np.float32)
        pos = np.ascontiguousarray(np.asarray(inputs["positions"][b], np.int32)[:S].reshape(S // 128, 128).T)
        m = dict(shared)
        m["x"] = np.ascontiguousarray(np.asarray(inputs["x"][b], np.float32)[:S])
        m["pos"] = pos
        m["smalls"] = np.ascontiguousarray(smalls)
        maps.append(m)
    return maps


def kernel(**inputs):
    B, S = inputs["x"].shape[0], inputs["x"].shape[1]
    nc, _ = build(S)
    maps = make_in_maps(inputs, S, list(range(B)))
    res = run_bass_kernel_spmd(nc, maps, core_ids=list(range(B)))
    return np.stack([np.asarray(r["out"], np.float32) for r in res.results], axis=0)
```
